# Optimizing a Trainium2 kernel written in Bass

```python
import jax, jax.numpy as jnp
from jax import lax
import numpy as np

D_MODEL = 2048
BATCH = 1
SEQ = 16384
DEPTH = 2

CHUNK = 64
D_MIX = D_MODEL
D_CONV = D_MIX // 2
D_GLA = D_MIX - D_CONV
N_GLA_HEADS = 4
HEAD_V = D_GLA // N_GLA_HEADS
HEAD_K = HEAD_V // 2
D_QK = N_GLA_HEADS * HEAD_K
GATE_RANK = 16
GATE_TAU = 16.0
CONV_WIDTH = 3
D_FF = ((int(8 * D_MODEL / 3) + 255) // 256) * 256
FFN_RES_SCALE = 0.5
EPS = 1e-6
IN_COLS = 3 * D_CONV + 2 * D_QK + 2 * D_GLA + GATE_RANK

kernel_name = "hymba_conv_gla_macaron_trunk"


def _rmsnorm(x, g):
    xf = x.astype(jnp.float32)
    y = xf * lax.rsqrt(jnp.mean(xf * xf, axis=-1, keepdims=True) + EPS)
    return (y * g.astype(jnp.float32)).astype(x.dtype)


def _swiglu(h, w_gate, w_up, w_down):
    return (jax.nn.silu(h @ w_gate) * (h @ w_up)) @ w_down


def _causal_dwconv(u, w):
    c = u.shape[-1]
    return lax.conv_general_dilated(
        u, w[:, None, :].astype(u.dtype), window_strides=(1,),
        padding=[(CONV_WIDTH - 1, 0)], dimension_numbers=("NWC", "WIO", "NWC"),
        feature_group_count=c)


def _gla_chunk_causal(q, k, v, log_a):
    b, t, h, _ = q.shape
    nc = t // CHUNK

    def to_chunks(a):
        return a.reshape(b, nc, CHUNK, h, a.shape[-1]).transpose(1, 0, 3, 2, 4).astype(jnp.float32)

    qc, kc, vc, ac = to_chunks(q), to_chunks(k), to_chunks(v), to_chunks(log_a)

    def step(state, inp):
        qi, ki, vi, ai = inp
        bcum = jnp.cumsum(ai, axis=2)
        o_inter = jnp.einsum("bhtk,bhkv->bhtv", qi * jnp.exp(bcum), state)
        decay = jnp.exp(-jnp.abs(bcum[:, :, :, None, :] - bcum[:, :, None, :, :]))
        scores = jnp.einsum("bhtk,bhsk,bhtsk->bhts", qi, ki, decay)
        o_intra = jnp.einsum("bhts,bhsv->bhtv", scores, vi)
        blast = bcum[:, :, -1:, :]
        new_state = jnp.exp(blast[:, :, 0, :])[..., None] * state + jnp.einsum(
            "bhsk,bhsv->bhkv", ki * jnp.exp(blast - bcum), vi)
        return new_state, o_inter + o_intra

    s0 = jnp.zeros((b, h, q.shape[-1], v.shape[-1]), jnp.float32)
    _, out = lax.scan(step, s0, (qc, kc, vc, ac))
    return out.transpose(1, 0, 3, 2, 4).reshape(b, t, h, v.shape[-1])


def _token_mixing(h, w_in, conv_w, gate_w2, gate_b, gla_norm, w_out):
    b, t, _ = h.shape
    proj = h @ w_in
    sizes = (D_CONV, D_CONV, D_CONV, D_QK, D_QK, D_GLA, D_GLA, GATE_RANK)
    cuts = [int(c) for c in np.cumsum(sizes)[:-1]]
    xv, gb, gc, q, k, v, g, zl = jnp.split(proj, cuts, axis=-1)

    y_conv = gb * _causal_dwconv(gc * xv, conv_w)

    log_a = jax.nn.log_sigmoid((zl @ gate_w2 + gate_b).astype(jnp.float32)) / GATE_TAU
    qh = q.reshape(b, t, N_GLA_HEADS, HEAD_K) * (HEAD_K ** -0.5)
    kh = k.reshape(b, t, N_GLA_HEADS, HEAD_K)
    vh = v.reshape(b, t, N_GLA_HEADS, HEAD_V)
    o = _gla_chunk_causal(qh, kh, vh, log_a.reshape(b, t, N_GLA_HEADS, HEAD_K))
    o = _rmsnorm(o, gla_norm).astype(h.dtype).reshape(b, t, D_GLA)
    y_gla = o * jax.nn.silu(g)

    return jnp.concatenate([y_conv, y_gla], axis=-1) @ w_out


def setup_inputs(seed: int = 0) -> dict:
    key = jax.random.key(seed)
    ks = jax.random.split(key, 20)
    f32 = jnp.float32

    def nrm(k, shape, fan_in):
        return jax.random.normal(k, shape, f32) * (fan_in ** -0.5)

    def gain(k, shape):
        return 1.0 + 0.02 * jax.random.normal(k, shape, f32)

    return {
        "x": jax.random.normal(ks[0], (BATCH, SEQ, D_MODEL), f32),
        "ffn1_norm": gain(ks[1], (DEPTH, D_MODEL)),
        "ffn1_w_gate": nrm(ks[2], (DEPTH, D_MODEL, D_FF), D_MODEL),
        "ffn1_w_up": nrm(ks[3], (DEPTH, D_MODEL, D_FF), D_MODEL),
        "ffn1_w_down": nrm(ks[4], (DEPTH, D_FF, D_MODEL), D_FF),
        "mix_norm": gain(ks[5], (DEPTH, D_MODEL)),
        "w_in": nrm(ks[6], (DEPTH, D_MODEL, IN_COLS), D_MODEL),
        "conv_w": nrm(ks[7], (DEPTH, CONV_WIDTH, D_CONV), CONV_WIDTH),
        "gate_w2": nrm(ks[8], (DEPTH, GATE_RANK, D_QK), GATE_RANK),
        "gate_b": 0.01 * jax.random.normal(ks[9], (DEPTH, D_QK), f32),
        "gla_norm": gain(ks[10], (DEPTH, HEAD_V)),
        "w_out": nrm(ks[11], (DEPTH, D_MIX, D_MODEL), D_MIX),
        "ffn2_norm": gain(ks[12], (DEPTH, D_MODEL)),
        "ffn2_w_gate": nrm(ks[13], (DEPTH, D_MODEL, D_FF), D_MODEL),
        "ffn2_w_up": nrm(ks[14], (DEPTH, D_MODEL, D_FF), D_MODEL),
        "ffn2_w_down": nrm(ks[15], (DEPTH, D_FF, D_MODEL), D_FF),
        "final_norm": gain(ks[16], (D_MODEL,)),
    }


def reference(x, ffn1_norm, ffn1_w_gate, ffn1_w_up, ffn1_w_down, mix_norm, w_in, conv_w,
              gate_w2, gate_b, gla_norm, w_out, ffn2_norm, ffn2_w_gate, ffn2_w_up,
              ffn2_w_down, final_norm):
    for l in range(DEPTH):
        x = x + FFN_RES_SCALE * _swiglu(_rmsnorm(x, ffn1_norm[l]),
                                        ffn1_w_gate[l], ffn1_w_up[l], ffn1_w_down[l])
        x = x + _token_mixing(_rmsnorm(x, mix_norm[l]), w_in[l], conv_w[l], gate_w2[l],
                              gate_b[l], gla_norm[l], w_out[l])
        x = x + FFN_RES_SCALE * _swiglu(_rmsnorm(x, ffn2_norm[l]),
                                        ffn2_w_gate[l], ffn2_w_up[l], ffn2_w_down[l])
    return _rmsnorm(x, final_norm)
```

```python
import numpy as np
import concourse.bass as bass
import concourse.mybir as mybir
from concourse.bass_utils import run_bass_kernel_spmd

F32 = mybir.dt.float32
BF16 = mybir.dt.bfloat16
AF = mybir.ActivationFunctionType
ALU = mybir.AluOpType

NCORES = 8
D = 2048
T = 2048
L = 2
DFF = 5632
NJ = DFF // 128
KT = D // 128
EPS = 1e-6
NWIN = 49
QSCALE = 128 ** -0.5


class Ev:
    __slots__ = ("sem", "val", "eng")

    def __init__(self, sem, val, eng):
        self.sem = sem
        self.val = val
        self.eng = eng


class Region:
    __slots__ = ("w", "r")

    def __init__(self):
        self.w = None
        self.r = []


class Sched:
    ENGS = ("pe", "dve", "act", "pool", "sp")

    def __init__(self, nc):
        self.nc = nc
        self.sem = {e: nc.alloc_semaphore("es_" + e) for e in self.ENGS}
        self.cnt = {e: 0 for e in self.ENGS}
        self.streams = {e: [] for e in self.ENGS}
        self.waited = {e: {} for e in self.ENGS}
        self.regions = {}
        self.pending = {e: [] for e in self.ENGS}
        self.dma_evs = []

    def reg(self, key):
        r = self.regions.get(key)
        if r is None:
            r = self.regions[key] = Region()
        return r

    def _collect(self, eng, reads, writes):
        deps = []
        for k in reads:
            r = self.reg(k)
            if r.w is not None:
                deps.append(r.w)
        for k in writes:
            r = self.reg(k)
            if r.w is not None:
                deps.append(r.w)
            deps.extend(r.r)
        best = {}
        for ev in deps:
            if ev.eng == "pe" and eng == "pe":
                continue
            if ev.val is None:
                raise RuntimeError("dependency on unresolved op")
            key = id(ev.sem)
            if key not in best or best[key].val < ev.val:
                best[key] = ev
        waits = []
        wd = self.waited[eng]
        for key, ev in best.items():
            if wd.get(key, 0) >= ev.val:
                continue
            wd[key] = ev.val
            waits.append((ev.sem, ev.val))
        return waits

    def _commit(self, ev, reads, writes):
        for k in reads:
            self.reg(k).r.append(ev)
        for k in writes:
            r = self.reg(k)
            r.w = ev
            r.r = []

    def op(self, eng, fn, reads=(), writes=(), inc=True):
        waits = self._collect(eng, reads, writes)
        if inc:
            self.cnt[eng] += 1
            ev = Ev(self.sem[eng], self.cnt[eng], eng)
            for p in self.pending[eng]:
                p.val = ev.val
            self.pending[eng] = []
            incs = [(self.sem[eng], 1)]
        else:
            ev = Ev(self.sem[eng], None, eng)
            self.pending[eng].append(ev)
            incs = []
        self.streams[eng].append((waits, fn, incs))
        self._commit(ev, reads, writes)
        return ev

    def dma(self, eng, fn, dstate, reads=(), writes=()):
        waits = self._collect(eng, reads, writes)
        dstate[1] += 16
        ev = Ev(dstate[0], dstate[1], "dma")
        self.streams[eng].append((waits, fn, [(dstate[0], 16)]))
        self._commit(ev, reads, writes)
        return ev

    def wait_evs(self, eng, evs):
        self.streams[eng].append(([(ev.sem, ev.val) for ev in evs], None, []))

    def emit(self):
        handles = {"pe": "tensor", "dve": "vector", "act": "scalar", "pool": "gpsimd", "sp": "sync"}
        with self.nc.Block() as block:
            for e in self.ENGS:
                stream = self.streams[e]

                def body(engh, stream=stream):
                    for waits, fn, incs in stream:
                        for (s, v) in waits:
                            engh.wait_ge(s, v)
                        if fn is None:
                            continue
                        ins = fn(engh)
                        for (s, v) in incs:
                            ins = ins.then_inc(s, v)

                getattr(block, handles[e])(body)


class Prog:
    def __init__(self, cfg):
        self.cfg = cfg
        steps = self.steps = list(cfg["steps"])
        nc = self.nc = bass.Bass("TRN2", target_bir_lowering=False)
        self.S = Sched(nc)
        dt_in = lambda name, shape: nc.dram_tensor(name, shape, F32, kind="ExternalInput").ap()
        dt_out = lambda name, shape: nc.dram_tensor(name, shape, F32, kind="ExternalOutput").ap()
        self.d_xT = dt_in("xT", [128, KT, T])
        self.d_w = {}
        for name, shape in needed_weights(steps):
            self.d_w[name] = dt_in(name, shape)
        self.d_small = dt_in("small", [128, 256])
        self.d_gw2 = dt_in("gw2", [33, L * 512])
        self.d_cst = dt_in("cst", [128, 4 * 128])
        self.d_cm = dt_in("cm", [128, 24])
        self.XW = 1056
        self.has_final = "final" in steps
        self.has_pre = any(st.startswith("pre_") for st in steps)
        self.has_comb = any(st.startswith("comb_") for st in steps)
        self.fused = "xchg" in steps
        if self.has_final:
            self.d_out = dt_out("outT", [128, KT, T])
        else:
            self.d_xdump = dt_out("xT_out", [128, KT, T])
        if self.fused:
            self.d_xin = nc.dram_tensor("xch_in", [128, self.XW], F32, kind="Internal").ap()
            self.d_xall = nc.dram_tensor("xch_out", [NCORES * 128, self.XW], F32, kind="Internal").ap()
        else:
            if self.has_pre:
                self.d_xin = dt_out("xch", [128, self.XW])
            if self.has_comb:
                self.d_xall = dt_in("xall", [NCORES * 128, self.XW])

        self.xT = nc.alloc_sbuf_tensor("xT_sb", [128, KT, T], F32)
        SCRN = 19968
        self.scr = nc.alloc_sbuf_tensor("scr", [128, SCRN], F32)
        self.scr_n = SCRN
        self.PS = [nc.alloc_psum_tensor(f"ps{i}", [128, 512], F32) for i in range(8)]
        self._off = 0
        self.NS = 5
        self.slot_rel = [-1] * 5
        self.slots = [self.carve_bf(2048) for _ in range(self.NS)]
        self.slot_d = [[nc.alloc_semaphore(f"ds{i}"), 0] for i in range(self.NS)]
        self.Sst = self.carve_f(1024).rearrange("p (h v) -> p h v", h=4)
        self.Sbf = [self.carve_bf(256) for _ in range(2)]
        self.Atot = self.carve_f(4)
        self.uhalo = self.carve_f(16)
        self.ulast = self.carve_f(16)
        self.small = self.carve_f(256)
        self.cm = self.carve_f(24)
        self.ones2048 = self.carve_bf(128)
        self.ones256 = self.carve_bf(128)
        self.sq = [self.carve_bf(512) for _ in range(2)]
        self.rstd = self.carve_f(512)
        self.union0 = self._off
        self.misc_d = [nc.alloc_semaphore("dmisc"), 0]
        self.all_d = list(self.slot_d) + [self.misc_d]
        self.items = []
        self.slot_rr = 0

    def new_dsem(self, name):
        st = [self.nc.alloc_semaphore(name), 0]
        self.all_d.append(st)
        return st

    def barrier(self):
        S = self.S
        for e in S.ENGS:
            waits = []
            for o in S.ENGS:
                if o != e and S.cnt[o] > 0:
                    if S.pending[o]:
                        raise RuntimeError("barrier with unresolved ops on " + o)
                    waits.append((S.sem[o], S.cnt[o]))
                    S.waited[e][id(S.sem[o])] = max(S.waited[e].get(id(S.sem[o]), 0), S.cnt[o])
            for st in self.all_d:
                if st[1] > 0:
                    waits.append((st[0], st[1]))
                    S.waited[e][id(st[0])] = max(S.waited[e].get(id(st[0]), 0), st[1])
            S.streams[e].append((waits, None, []))

    def carve_f(self, n):
        a = self._off
        self._off += (n + 7) // 8 * 8
        assert self._off <= self.scr_n, ("scratch overflow", self._off)
        return self.scr[:, a:a + n]

    def carve_bf(self, n):
        w = (n + 1) // 2
        a = self._off
        self._off += (w + 7) // 8 * 8
        assert self._off <= self.scr_n, ("scratch overflow", self._off)
        return self.scr[:, a:a + w].bitcast(BF16)

    def mm(self, out, lhsT, rhs, start, stop, reads, writes, inc=False):
        return self.S.op("pe", lambda e: e.matmul(out, lhsT=lhsT, rhs=rhs, start=start, stop=stop),
                         reads, writes, inc)

    def act(self, out, in_, func, reads, writes, scale=1.0, bias=0.0):
        return self.S.op("act", lambda e: e.activation(out=out, in_=in_, func=func, bias=bias, scale=scale),
                         reads, writes)

    def tt(self, out, in0, in1, op, reads, writes, eng="dve"):
        return self.S.op(eng, lambda e: e.tensor_tensor(out=out, in0=in0, in1=in1, op=op), reads, writes)

    def ts(self, out, in0, s1, s2, op0, op1, reads, writes, eng="dve"):
        if s2 is None:
            return self.S.op(eng, lambda e: e.tensor_scalar(out=out, in0=in0, scalar1=s1, scalar2=None, op0=op0),
                             reads, writes)
        return self.S.op(eng, lambda e: e.tensor_scalar(out=out, in0=in0, scalar1=s1, scalar2=s2, op0=op0, op1=op1),
                         reads, writes)

    def stt(self, out, in0, scalar, in1, op0, op1, reads, writes, eng="dve"):
        return self.S.op(eng, lambda e: e.scalar_tensor_tensor(out=out, in0=in0, scalar=scalar, in1=in1,
                                                               op0=op0, op1=op1), reads, writes)

    def cp(self, out, in_, reads, writes, eng="dve"):
        if eng == "act":
            return self.act(out, in_, AF.Copy, reads, writes)
        return self.S.op(eng, lambda e: e.tensor_copy(out=out, in_=in_), reads, writes)

    def memset(self, ap, val, writes, eng="dve"):
        return self.S.op(eng, lambda e: e.memset(ap, val), (), writes)

    def dma_sp(self, out, in_, reads, writes):
        return self.S.dma("sp", lambda e: e.dma_start(out=out, in_=in_), self.misc_d, reads, writes)

    def item(self, src, fn, hold=0):
        self.items.append((src, fn, hold))

    def run_items(self, depth=4):
        items = self.items
        self.items = []
        widx = [i for i in range(len(items)) if items[i][0] is not None]
        nW = len(widx)
        slot_of = {}
        nxt = [0]

        def issue(k):
            while nxt[0] < min(k + depth + 1, nW):
                m = nxt[0]
                si = self.slot_rr % self.NS
                if self.slot_rel[si] >= k:
                    break
                src, _, hold = items[widx[m]]
                self.slot_rr += 1
                self.slot_rel[si] = m + hold
                slot_of[m] = si
                sl = self.slots[si]
                self.S.dma("pool", lambda e, sl=sl, src=src: e.dma_start(out=sl, in_=src),
                           self.slot_d[si], (), [("slot", si)])
                nxt[0] += 1

        k = 0
        for i in range(len(items)):
            src, fn, hold = items[i]
            if src is not None:
                issue(k)
                assert k in slot_of, "prefetch deadlock"
                si = slot_of[k]
                fn(self.slots[si], ("slot", si))
                k += 1
            else:
                fn(None, None)
        self.slot_rel = [-1] * self.NS

    def prologue(self):
        S = self.S
        evs = []
        for kt in range(KT):
            evs.append(S.dma("sp", lambda e, kt=kt: e.dma_start(out=self.xT[:, kt, :], in_=self.d_xT[:, kt, :]),
                             self.misc_d, (), [("x", kt, s) for s in range(4)]))
        evs.append(self.dma_sp(self.small, self.d_small, (), ["small"]))
        evs.append(self.dma_sp(self.cm, self.d_cm, (), ["cm"]))
        for ev in evs:
            ev.val = self.misc_d[1]
        self.memset(self.ones2048, 1.0 / 2048.0, ["ones2048"])
        self.memset(self.ones256, 1.0 / 256.0, ["ones256"])

    def norm_sub(self, gcol, c0, hdst, hkey):
        xT = self.xT
        s = c0 // 512
        ss = self.PS[7]
        for kt in range(KT):
            sq = self.sq[kt % 2]
            self.act(sq, xT[:, kt, c0:c0 + 512], AF.Square, [("x", kt, s)], [("sq", kt % 2)])
            self.mm(ss[:, :], self.ones2048, sq, kt == 0, kt == KT - 1,
                    [("sq", kt % 2), "ones2048"], [("ps", 7)], inc=True)
        self.act(self.rstd, ss[:, :], AF.Sqrt, [("ps", 7)], ["rstd"], bias=EPS)
        self.S.op("dve", lambda e: e.reciprocal(out=self.rstd, in_=self.rstd), ["rstd"], ["rstd"])
        for kt in range(KT):
            self.stt(hdst(kt), xT[:, kt, c0:c0 + 512], self.small[:, gcol + kt:gcol + kt + 1], self.rstd,
                     ALU.mult, ALU.mult, [("x", kt, s), "rstd", "small"], [hkey(kt)])

    def ffn(self, l, f):
        self.barrier()
        base = self.union0
        self._off = base
        hT = self.carve_bf(KT * 1024).rearrange("p (k t) -> p k t", k=KT)
        actb = self.carve_bf(4 * 1024).rearrange("p (j t) -> p j t", j=4)
        stmp = [self.carve_f(512) for _ in range(2)]
        gcol = (0 if f == 0 else 64) + l * 16
        PS = self.PS
        xT = self.xT
        d_wgu = self.d_w[f"wgu_{l}_{f}"]
        d_wd = self.d_w[f"wd_{l}_{f}"]
        for tile in range(2):
            t0 = tile * 1024

            def do_norm(_a, _b, t0=t0):
                for sub in range(2):
                    self.norm_sub(gcol, t0 + sub * 512,
                                  lambda kt, sub=sub: hT[:, kt, sub * 512:(sub + 1) * 512],
                                  lambda kt, sub=sub: ("h", kt, sub))
            self.item(None, do_norm)
            cnt = [0]
            for g in range(NJ // 4):
                for jj in range(4):
                    j = g * 4 + jj
                    for which in range(2):
                        def gu(slot, skey, jj=jj, which=which):
                            w = slot.rearrange("p (k c) -> p k c", k=KT)
                            for sub in range(2):
                                ps = PS[which * 2 + sub]
                                pk = ("ps", which * 2 + sub)
                                for kt in range(KT):
                                    self.mm(ps[:, :], w[:, kt, :], hT[:, kt, sub * 512:(sub + 1) * 512],
                                            kt == 0, kt == KT - 1, [skey, ("h", kt, sub)], [pk],
                                            inc=(kt == KT - 1))
                                if which == 0:
                                    self.act(stmp[sub], ps[:, :], AF.Silu, [pk], [("stmp", sub)])
                                else:
                                    self.tt(actb[:, jj, sub * 512:(sub + 1) * 512], ps[:, :], stmp[sub], ALU.mult,
                                            [pk, ("stmp", sub)], [("act", jj, sub)])
                        self.item(d_wgu[j * 2 + which], gu)
                pair = {}
                for jj in range(4):
                    def dn(slot, skey, jj=jj, g=g, t0=t0, pair=pair):
                        pair[jj] = (slot, skey)
                        if jj != 3:
                            return
                        for dc in range(KT):
                            for sub in range(2):
                                bi = 4 + (cnt[0] % 3)
                                cnt[0] += 1
                                ps = PS[bi]
                                pk = ("ps", bi)
                                for q in range(4):
                                    sq_, kq = pair[q]
                                    self.mm(ps[:, :], sq_[:, dc * 128:(dc + 1) * 128],
                                            actb[:, q, sub * 512:(sub + 1) * 512], q == 0, q == 3,
                                            [kq, ("act", q, sub)], [pk], inc=(q == 3))
                                c0 = t0 + sub * 512
                                xs = xT[:, dc, c0:c0 + 512]
                                self.stt(xs, ps[:, :], 0.5, xs, ALU.mult, ALU.add,
                                         [pk, ("x", dc, c0 // 512)], [("x", dc, c0 // 512)])
                    self.item(d_wd[g * 4 + jj], dn, hold=3 - jj)
        self.run_items()

    @staticmethod
    def pq(bank, *qs):
        return [("ps", bank)]

    def mix_carve(self):
        self._off = self.union0
        M = {}
        M["hT"] = self.carve_bf(KT * 512).rearrange("p (k t) -> p k t", k=KT)
        M["y"] = self.carve_bf(8 * 512).rearrange("p (j t) -> p j t", j=8)
        M["zlT"] = self.carve_f(512)
        M["gw2l"] = self.carve_f(512)
        M["cst"] = self.carve_f(512)
        M["qT"] = self.carve_bf(512)
        M["kT"] = self.carve_bf(512)
        M["ktok"] = self.carve_f(512).rearrange("p (b k) -> p b k", b=4)
        M["vtok"] = self.carve_bf(1024).rearrange("p (b v) -> p b v", b=4)
        for nm in ("e", "sp", "Ep", "Em", "ER", "tmpA", "tmpB", "rstd2", "t1"):
            M[nm] = self.carve_f(128)
        for nm in ("Qp", "Qm", "Kp", "Km", "Kpr", "sc", "sq2a", "sq2b"):
            M[nm] = self.carve_bf(128)
        M["aeff"] = self.carve_f(4)
        g0 = self._off
        M["u"] = self.carve_f(520)
        M["c"] = self.carve_f(512)
        M["cpad"] = self.carve_f(32)
        M["G"] = self.scr[:, g0:g0 + self.XW]
        return M

    def mix_setup(self, l, M):
        d = self.new_dsem(f"dmx{len(self.all_d)}")
        e1 = self.S.dma("sp", lambda e: e.dma_start(out=M["gw2l"][0:33, :], in_=self.d_gw2[:, l * 512:(l + 1) * 512]),
                        d, (), ["gw2l"])
        e2 = self.S.dma("sp", lambda e: e.dma_start(out=M["cst"], in_=self.d_cst), d, (), ["cst"])
        e1.val = e2.val = d[1]
        self.memset(M["zlT"][0:33, :], 0.0, ["zlT"])
        self.memset(M["zlT"][32:33, :], 1.0, ["zlT"])

    def proj_fm(self, M, bank, slot, skey, mcols=128):
        w = slot.rearrange("p (k c) -> p k c", k=KT)
        ps = self.PS[bank]
        for kt in range(KT):
            self.mm(ps[0:mcols, :], w[:, kt, 0:mcols], M["hT"][:, kt, :], kt == 0, kt == KT - 1,
                    [skey, ("h", kt, 0)], self.pq(bank), inc=(kt == KT - 1))

    def proj_tm(self, M, bank, slot, skey):
        w = slot.rearrange("p (k c) -> p k c", k=KT)
        ps = self.PS[bank]
        for blk in range(4):
            for kt in range(KT):
                self.mm(ps[:, blk * 128:(blk + 1) * 128], M["hT"][:, kt, blk * 128:(blk + 1) * 128], w[:, kt, :],
                        kt == 0, kt == KT - 1, [skey, ("h", kt, 0)], self.pq(bank, blk), inc=(kt == KT - 1))

    def gla_block(self, l, h, blk, M, main):
        PS = self.PS
        tb = blk * 128
        cst = M["cst"]
        U2, L2, MA, MB = cst[:, 0:128], cst[:, 128:256], cst[:, 256:384], cst[:, 384:512]
        z, ss = PS[3][:, 0:128], PS[3][:, 128:256]
        bc, A = PS[7][:, 0:128], PS[7][:, 128:256]
        R, B = PS[0][:, 0:128], PS[0][:, 128:256]
        o = [PS[6][:, 0:128], PS[1][:, 0:128]]
        obank = [6, 1]
        SU = PS[5][:, 0:256]
        Sh = self.Sst[:, h, :]
        self.mm(z, M["zlT"][0:33, tb:tb + 128], M["gw2l"][0:33, h * 128:(h + 1) * 128], True, True,
                ["zlT", "gw2l"], self.pq(3), inc=True)
        self.act(M["e"], z, AF.Exp, self.pq(3), ["e"], scale=-1.0)
        self.act(M["sp"], M["e"], AF.Ln, ["e"], ["sp"], bias=1.0)
        self.mm(bc, M["sp"], U2, True, True, ["sp", "cst"], self.pq(7), inc=True)
        self.mm(R, L2, M["sp"], True, True, ["sp", "cst"], self.pq(0), inc=True)
        self.act(M["Ep"], bc, AF.Exp, self.pq(7), ["Ep"])
        self.act(M["ER"], R, AF.Exp, self.pq(0), ["ER"])
        self.tt(M["Kpr"], M["ktok"][:, blk, :], M["ER"], ALU.mult, ["ktok", "ER"], ["Kpr"])
        eb = [M["Ep"][:, 63:64], M["Ep"][:, 127:128]]
        if main:
            self.act(M["Em"], bc, AF.Exp, self.pq(7), ["Em"], scale=-1.0)
            qb, kb = M["qT"][:, tb:tb + 128], M["kT"][:, tb:tb + 128]
            self.stt(M["Qp"], qb, QSCALE, M["Ep"], ALU.mult, ALU.mult, ["qT", "Ep"], ["Qp"])
            self.stt(M["Qm"], qb, QSCALE, M["Em"], ALU.mult, ALU.mult, ["qT", "Em"], ["Qm"])
            self.tt(M["Kp"], kb, M["Ep"], ALU.mult, ["kT", "Ep"], ["Kp"])
            self.tt(M["Km"], kb, M["Em"], ALU.mult, ["kT", "Em"], ["Km"])
            self.mm(A, M["Km"], M["Qp"], True, True, ["Km", "Qp"], self.pq(7), inc=True)
            self.mm(B, M["Kp"], M["Qm"], True, True, ["Kp", "Qm"], self.pq(0), inc=True)
            self.tt(M["tmpA"], A, MA, ALU.mult, self.pq(7) + ["cst"], ["tmpA"])
            self.tt(M["tmpB"], B, MB, ALU.mult, self.pq(0) + ["cst"], ["tmpB"])
            self.tt(M["sc"], M["tmpA"], M["tmpB"], ALU.add, ["tmpA", "tmpB"], ["sc"])
            for vh in range(2):
                self.mm(o[vh], M["vtok"][:, blk, vh * 128:(vh + 1) * 128], M["sc"], True, False,
                        ["vtok", "sc"], self.pq(obank[vh]))
                self.mm(o[vh][:, 0:64], self.Sbf[0][:, vh * 128:(vh + 1) * 128], M["Qp"][:, 0:64], False, False,
                        [("Sbf", 0), "Qp"], self.pq(obank[vh]))
        self.mm(SU, M["Kpr"][0:64, :], M["vtok"][0:64, blk, :], True, True, ["Kpr", "vtok"], self.pq(5), inc=True)
        self.stt(Sh, Sh, eb[0], SU, ALU.mult, ALU.add, [("S", h), "Ep"] + self.pq(5), [("S", h)])
        if main:
            self.cp(self.Sbf[1], Sh, [("S", h)], [("Sbf", 1)], eng="act")
            for vh in range(2):
                self.mm(o[vh][:, 64:128], self.Sbf[1][:, vh * 128:(vh + 1) * 128], M["Qp"][:, 64:128], False, True,
                        [("Sbf", 1), "Qp"], self.pq(obank[vh]), inc=True)
        self.mm(SU, M["Kpr"][64:128, :], M["vtok"][64:128, blk, :], True, True, ["Kpr", "vtok"], self.pq(5), inc=True)
        self.stt(Sh, Sh, eb[1], SU, ALU.mult, ALU.add, [("S", h), "Ep"] + self.pq(5), [("S", h)])
        if main:
            self.cp(self.Sbf[0], Sh, [("S", h)], [("Sbf", 0)], eng="act")
            sq2 = [M["sq2a"], M["sq2b"]]
            for vh in range(2):
                self.act(sq2[vh], o[vh], AF.Square, self.pq(obank[vh]), [("sq2", vh)])
                self.mm(ss, self.ones256, sq2[vh], vh == 0, vh == 1, [("sq2", vh), "ones256"], self.pq(3), inc=(vh == 1))
            self.act(M["rstd2"], ss, AF.Sqrt, self.pq(3), ["rstd2"], bias=EPS)
            self.S.op("dve", lambda e: e.reciprocal(out=M["rstd2"], in_=M["rstd2"]), ["rstd2"], ["rstd2"])
            for vh in range(2):
                gcol = 160 + l * 2 + vh
                self.stt(M["t1"], o[vh], self.small[:, gcol:gcol + 1], M["rstd2"], ALU.mult, ALU.mult,
                         self.pq(obank[vh]) + ["rstd2", "small"], ["t1"])
                yv = M["y"][:, h * 2 + vh, tb:tb + 128]
                self.tt(yv, M["t1"], yv, ALU.mult, ["t1", ("y", h * 2 + vh)], [("y", h * 2 + vh)])
        else:
            At = self.Atot[:, h:h + 1]
            self.ts(At, At, eb[0], eb[1], ALU.mult, ALU.mult, ["Atot", "Ep"], ["Atot"])

    def mix_norm_item(self, l, t0, M):
        def f(_a, _b):
            self.norm_sub(32 + l * 16, t0, lambda kt: M["hT"][:, kt, :], lambda kt: ("h", kt, 0))
        self.item(None, f)

    def zl_item(self, l, M):
        d_win = self.d_w[f"win_{l}"]

        def f(slot, skey):
            self.proj_fm(M, 5, slot, skey, mcols=16)
            self.cp(M["zlT"][0:16, :], self.PS[5][0:16, :], self.pq(5), ["zlT"], eng="act")
        self.item(d_win[48], f)

    def prepass(self, l):
        self.barrier()
        M = self.mix_carve()
        self.mix_setup(l, M)
        d_win = self.d_w[f"win_{l}"]
        for h in range(4):
            self.memset(self.Sst[:, h, :], 0.0, [("S", h)])
        self.memset(self.Atot, 1.0, ["Atot"])
        for ti in range(4):
            t0 = ti * 512
            self.mix_norm_item(l, t0, M)
            self.zl_item(l, M)
            for h in range(4):
                def fk(slot, skey):
                    self.proj_tm(M, 4, slot, skey)
                    self.cp(M["ktok"], self.PS[4][:, :].rearrange("p (b k) -> p b k", b=4), self.pq(4), ["ktok"])
                self.item(d_win[28 + h], fk)
                for vh in range(2):
                    def fv(slot, skey, vh=vh, h=h):
                        bank = 4 if vh == 0 else 6
                        self.proj_tm(M, bank, slot, skey)
                        self.cp(M["vtok"][:, :, vh * 128:(vh + 1) * 128],
                                self.PS[bank][:, :].rearrange("p (b v) -> p b v", b=4), self.pq(bank), ["vtok"], eng="act")
                        if vh == 1:
                            for blk in range(4):
                                self.gla_block(l, h, blk, M, main=False)
                    self.item(d_win[32 + 2 * h + vh], fv)
            if ti == 3:
                for j in range(8):
                    def fx(slot, skey, j=j):
                        w = slot.rearrange("p (k c) -> p k c", k=KT)
                        for kt in range(KT):
                            self.mm(self.PS[0][:, 0:2], w[:, kt, :], M["hT"][:, kt, 510:512], kt == 0, kt == KT - 1,
                                    [skey, ("h", kt, 0)], self.pq(0, 0), inc=(kt == KT - 1))
                        self.cp(M["cpad"][:, 0:2], self.PS[0][:, 0:2], self.pq(0, 0), ["cpad"], eng="act")
                    self.item(d_win[j], fx)

                    def fg(slot, skey, j=j):
                        w = slot.rearrange("p (k c) -> p k c", k=KT)
                        for kt in range(KT):
                            self.mm(self.PS[1][:, 0:2], w[:, kt, :], M["hT"][:, kt, 510:512], kt == 0, kt == KT - 1,
                                    [skey, ("h", kt, 0)], self.pq(1, 0), inc=(kt == KT - 1))
                        self.tt(self.ulast[:, 2 * j:2 * j + 2], self.PS[1][:, 0:2], M["cpad"][:, 0:2], ALU.mult,
                                self.pq(1, 0) + ["cpad"], ["ulast"])
                    self.item(d_win[16 + j], fg)
        self.run_items()
        zpad = M["cpad"][:, 8:20]
        self.memset(zpad, 0.0, ["zpad"])
        d = self.new_dsem(f"dxo{l}")
        evs = [
            self.S.dma("sp", lambda e: e.dma_start(out=self.d_xin[:, 1028:1032], in_=zpad[:, 0:4]), d, ["zpad"], ["xin"]),
            self.S.dma("sp", lambda e: e.dma_start(out=self.d_xin[:, 1048:1056], in_=zpad[:, 0:8]), d, ["zpad"], ["xin"]),
        ]
        evs += [
            self.S.dma("sp", lambda e: e.dma_start(out=self.d_xin[:, 0:1024], in_=self.Sst[:, :, :].rearrange("p h v -> p (h v)")),
                       d, [("S", h) for h in range(4)], ["xin"]),
            self.S.dma("sp", lambda e: e.dma_start(out=self.d_xin[:, 1024:1028], in_=self.Atot), d, ["Atot"], ["xin"]),
            self.S.dma("sp", lambda e: e.dma_start(out=self.d_xin[:, 1032:1048], in_=self.ulast), d, ["ulast"], ["xin"]),
        ]
        for ev in evs:
            ev.val = d[1]
        self.xch_evs = evs

    def xchg(self, l):
        d = self.new_dsem(f"dcc{l}")
        self.S.dma("pool", lambda e: e.collective_compute("AllGather", ALU.bypass, replica_groups=[list(range(NCORES))],
                                                          ins=[self.d_xin], outs=[self.d_xall]),
                   d, ["xin"], ["xall"])

    def combine(self, l):
        self.barrier()
        M = self.mix_carve()
        for h in range(4):
            self.memset(self.Sst[:, h, :], 0.0, [("S", h)])
        self.memset(self.uhalo, 0.0, ["uhalo"])
        cm = self.cm
        d = self.new_dsem(f"dcb{l}")
        G = M["G"]
        for j in range(NCORES - 1):
            self.S.dma("sp", lambda e, j=j: e.dma_start(out=G, in_=self.d_xall[j * 128:(j + 1) * 128, :]),
                       d, ["xall"], ["G"])
            mj, ej, omj = cm[:, j:j + 1], cm[:, 8 + j:9 + j], cm[:, 16 + j:17 + j]
            self.ts(M["aeff"], G[:, 1024:1028], mj, omj, ALU.mult, ALU.add, ["G", "cm"], ["aeff"])
            for h in range(4):
                Sh = self.Sst[:, h, :]
                ctmp = M["zlT"][:, 0:256]
                self.ts(ctmp, G[:, h * 256:(h + 1) * 256], mj, None, ALU.mult, None, ["G", "cm"], ["zlT"])
                self.stt(Sh, Sh, M["aeff"][:, h:h + 1], ctmp, ALU.mult, ALU.add,
                         [("S", h), "aeff", "zlT"], [("S", h)])
            self.stt(self.uhalo, G[:, 1032:1048], ej, self.uhalo, ALU.mult, ALU.add, ["G", "cm", "uhalo"], ["uhalo"])
        self.barrier()

    def mix(self, l):
        M = self.mix_carve()
        self.mix_setup(l, M)
        d_win = self.d_w[f"win_{l}"]
        d_wo = self.d_w[f"wo_{l}"]
        PS = self.PS
        xT = self.xT
        wcol = lambda tap, j: self.small[:, 112 + l * 24 + tap * 8 + j:112 + l * 24 + tap * 8 + j + 1]
        rot = [0]

        def wout_items(half, t0):
            for dc2 in range(8):
                def fo(slot, skey, dc2=dc2):
                    w = slot.rearrange("p (a k c) -> p a k c", a=2, k=8)
                    for pr in range(2):
                        dc = dc2 * 2 + pr
                        bank = 4 + (rot[0] % 2)
                        rot[0] += 1
                        for kt in range(8):
                            self.mm(PS[bank][:, :], w[:, pr, kt, :], M["y"][:, kt, :], kt == 0, kt == 7,
                                    [skey, ("y", kt)], self.pq(bank), inc=(kt == 7))
                        xs = xT[:, dc, t0:t0 + 512]
                        self.tt(xs, PS[bank][:, :], xs, ALU.add, self.pq(bank) + [("x", dc, t0 // 512)],
                                [("x", dc, t0 // 512)])
                self.item(d_wo[half * 8 + dc2], fo)

        for ti in range(4):
            t0 = ti * 512
            self.mix_norm_item(l, t0, M)
            for j in range(8):
                def fxv(slot, skey, j=j):
                    self.proj_fm(M, 0, slot, skey)
                    self.cp(M["c"], PS[0][:, :], self.pq(0), ["cbuf"], eng="act")
                self.item(d_win[j], fxv)

                def fgc(slot, skey, j=j):
                    self.proj_fm(M, 1, slot, skey)
                    u = M["u"]
                    self.cp(u[:, 0:2], self.uhalo[:, 2 * j:2 * j + 2], ["uhalo"], ["u"])
                    self.tt(u[:, 2:514], PS[1][:, :], M["c"], ALU.mult, self.pq(1) + ["cbuf"], ["u"])
                    self.cp(self.uhalo[:, 2 * j:2 * j + 2], u[:, 512:514], ["u"], ["uhalo"])
                    self.ts(M["c"], u[:, 0:512], wcol(0, j), None, ALU.mult, None, ["u", "small"], ["cbuf"])
                    self.stt(M["c"], u[:, 1:513], wcol(1, j), M["c"], ALU.mult, ALU.add, ["u", "small", "cbuf"], ["cbuf"])
                    self.stt(M["c"], u[:, 2:514], wcol(2, j), M["c"], ALU.mult, ALU.add, ["u", "small", "cbuf"], ["cbuf"])
                self.item(d_win[16 + j], fgc)

                def fgb(slot, skey, j=j):
                    self.proj_fm(M, 2, slot, skey)
                    self.tt(M["y"][:, j, :], PS[2][:, :], M["c"], ALU.mult, self.pq(2) + ["cbuf"], [("y", j)])
                self.item(d_win[8 + j], fgb)
            wout_items(0, t0)
            self.zl_item(l, M)
            for h in range(4):
                def fq(slot, skey):
                    self.proj_fm(M, 0, slot, skey)
                    self.cp(M["qT"], PS[0][:, :], self.pq(0), ["qT"], eng="act")
                self.item(d_win[24 + h], fq)

                def fk(slot, skey):
                    self.proj_fm(M, 1, slot, skey)
                    self.cp(M["kT"], PS[1][:, :], self.pq(1), ["kT"], eng="act")
                    self.proj_tm(M, 4, slot, skey)
                    self.cp(M["ktok"], PS[4][:, :].rearrange("p (b k) -> p b k", b=4), self.pq(4), ["ktok"])
                self.item(d_win[28 + h], fk)
                for vh in range(2):
                    def fv(slot, skey, vh=vh):
                        bank = 4 if vh == 0 else 6
                        self.proj_tm(M, bank, slot, skey)
                        self.cp(M["vtok"][:, :, vh * 128:(vh + 1) * 128],
                                PS[bank][:, :].rearrange("p (b v) -> p b v", b=4), self.pq(bank), ["vtok"], eng="act")
                    self.item(d_win[32 + 2 * h + vh], fv)
                for vh in range(2):
                    def fg(slot, skey, vh=vh, h=h):
                        self.proj_fm(M, 2, slot, skey)
                        self.act(M["y"][:, h * 2 + vh, :], PS[2][:, :], AF.Silu, self.pq(2), [("y", h * 2 + vh)])
                        if vh == 1:
                            self.cp(self.Sbf[0], self.Sst[:, h, :], [("S", h)], [("Sbf", 0)], eng="act")
                            for blk in range(4):
                                self.gla_block(l, h, blk, M, main=True)
                    self.item(d_win[40 + 2 * h + vh], fg)
            wout_items(1, t0)
        self.run_items()

    def final(self):
        self.barrier()
        self._off = self.union0
        ot = [self.carve_f(512) for _ in range(2)]
        gcol = 96
        n = 0
        evs = []
        xT = self.xT
        for sub in range(4):
            c0 = sub * 512
            ss = self.PS[7]
            for kt in range(KT):
                sq = self.sq[kt % 2]
                self.act(sq, xT[:, kt, c0:c0 + 512], AF.Square, [("x", kt, sub)], [("sq", kt % 2)])
                self.mm(ss[:, :], self.ones2048, sq, kt == 0, kt == KT - 1,
                        [("sq", kt % 2), "ones2048"], [("ps", 7)], inc=True)
            self.act(self.rstd, ss[:, :], AF.Sqrt, [("ps", 7)], ["rstd"], bias=EPS)
            self.S.op("dve", lambda e: e.reciprocal(out=self.rstd, in_=self.rstd), ["rstd"], ["rstd"])
            for kt in range(KT):
                o = ot[n % 2]
                self.stt(o, xT[:, kt, c0:c0 + 512], self.small[:, gcol + kt:gcol + kt + 1], self.rstd,
                         ALU.mult, ALU.mult, [("x", kt, sub), "rstd", "small"], [("ot", n % 2)])
                dst = self.d_out[:, kt, c0:c0 + 512]
                evs.append(self.S.dma("sp", lambda e, o=o, dst=dst: e.dma_start(out=dst, in_=o),
                                      self.out_d[n % 2], [("ot", n % 2)], []))
                n += 1
        return evs[-2:]

    def dump_x(self):
        self.barrier()
        d = self.new_dsem("dxd")
        evs = []
        for kt in range(KT):
            evs.append(self.S.dma("sp", lambda e, kt=kt: e.dma_start(out=self.d_xdump[:, kt, :], in_=self.xT[:, kt, :]),
                                  d, [("x", kt, s) for s in range(4)], []))
        for ev in evs:
            ev.val = d[1]
        return evs[-1:]

    def build(self):
        self.out_d = [self.new_dsem("dout0"), self.new_dsem("dout1")]
        self.xch_evs = []
        self.prologue()
        tail = []
        for st in self.steps:
            if st == "final":
                tail += self.final()
            elif st == "xchg":
                self.xchg(0)
            else:
                kind, l = st.rsplit("_", 1)
                l = int(l)
                if kind == "ffn1":
                    self.ffn(l, 0)
                elif kind == "ffn2":
                    self.ffn(l, 1)
                elif kind == "pre":
                    self.prepass(l)
                elif kind == "comb":
                    self.combine(l)
                elif kind == "mix":
                    self.mix(l)
                else:
                    raise ValueError(st)
        if not self.has_final:
            tail += self.dump_x()
        self.S.wait_evs("sp", tail + list(self.xch_evs if not self.fused else []))
        self.S.emit()
        return self.nc


LAUNCH_STEPS = [
    ["ffn1_0", "pre_0"],
    ["comb_0", "mix_0", "ffn2_0", "ffn1_1", "pre_1"],
    ["comb_1", "mix_1", "ffn2_1", "final"],
]
FUSED_STEPS = ["ffn1_0", "pre_0", "xchg", "comb_0", "mix_0", "ffn2_0",
               "ffn1_1", "pre_1", "xchg", "comb_1", "mix_1", "ffn2_1", "final"]


def needed_weights(steps):
    out = []
    seen = set()

    def add(name, shape):
        if name not in seen:
            seen.add(name)
            out.append((name, shape))
    for st in steps:
        if st in ("final", "xchg"):
            continue
        kind, l = st.rsplit("_", 1)
        if kind in ("ffn1", "ffn2"):
            f = 0 if kind == "ffn1" else 1
            add(f"wgu_{l}_{f}", [NJ * 2, 128, 2048])
            add(f"wd_{l}_{f}", [NJ, 128, 2048])
        elif kind == "pre":
            add(f"win_{l}", [NWIN, 128, 2048])
        elif kind == "mix":
            add(f"win_{l}", [NWIN, 128, 2048])
            add(f"wo_{l}", [16, 128, 2048])
    return out


def _granules_cols(W, ncols_pad=None):
    K, C = W.shape
    if ncols_pad is not None and ncols_pad != C:
        Wp = np.zeros((K, ncols_pad), np.float32)
        Wp[:, :C] = W
        W = Wp
        C = ncols_pad
    return np.ascontiguousarray(W.reshape(K // 128, 128, C // 128, 128).transpose(2, 1, 0, 3)).reshape(
        C // 128, 128, (K // 128) * 128)


def _prep_inputs(inp):
    f = lambda a: np.asarray(a, dtype=np.float32)
    W = {}
    for l in range(L):
        for fi, pre in enumerate(("ffn1", "ffn2")):
            gu = np.empty((NJ, 2, 128, 2048), np.float32)
            gu[:, 0] = _granules_cols(f(inp[pre + "_w_gate"][l]))
            gu[:, 1] = _granules_cols(f(inp[pre + "_w_up"][l]))
            W[f"wgu_{l}_{fi}"] = gu.reshape(NJ * 2, 128, 2048)
            W[f"wd_{l}_{fi}"] = np.ascontiguousarray(f(inp[pre + "_w_down"][l]).reshape(NJ, 128, 2048))
        W[f"win_{l}"] = _granules_cols(f(inp["w_in"][l]), NWIN * 128)
        g = _granules_cols(f(inp["w_out"][l])).reshape(16, 128, 16, 128)
        wo = np.empty((2, 8, 128, 2048), np.float32)
        for half in range(2):
            gh = g[:, :, half * 8:(half + 1) * 8, :]
            wo[half] = gh.reshape(8, 2, 128, 8, 128).transpose(0, 2, 1, 3, 4).reshape(8, 128, 2048)
        W[f"wo_{l}"] = wo.reshape(16, 128, 2048)
    small = np.zeros((128, 256), np.float32)
    pp = lambda v: f(v).reshape(-1, 128).T
    for l in range(L):
        small[:, l * 16:(l + 1) * 16] = pp(inp["ffn1_norm"][l])
        small[:, 32 + l * 16:32 + (l + 1) * 16] = pp(inp["mix_norm"][l])
        small[:, 64 + l * 16:64 + (l + 1) * 16] = pp(inp["ffn2_norm"][l])
        cw = f(inp["conv_w"][l])
        for tap in range(3):
            small[:, 112 + l * 24 + tap * 8:112 + l * 24 + (tap + 1) * 8] = cw[tap].reshape(8, 128).T
        small[:, 160 + l * 2:160 + (l + 1) * 2] = f(inp["gla_norm"][l]).reshape(2, 128).T
    small[:, 96:112] = pp(inp["final_norm"])
    gw2 = np.zeros((33, L * 512), np.float32)
    for l in range(L):
        gw2[0:16, l * 512:(l + 1) * 512] = f(inp["gate_w2"][l])
        gw2[32, l * 512:(l + 1) * 512] = f(inp["gate_b"][l])
    idx = np.arange(128)
    same = (idx[:, None] // 64) == (idx[None, :] // 64)
    U2 = np.where(same & (idx[:, None] <= idx[None, :]), -1.0 / 16.0, 0.0)
    L2 = np.where(same & (idx[:, None] > idx[None, :]), -1.0 / 16.0, 0.0)
    MA = np.where(same & (idx[:, None] <= idx[None, :]), 1.0, 0.0)
    MB = np.where(same & (idx[:, None] > idx[None, :]), 1.0, 0.0)
    cst = np.concatenate([U2, L2, MA, MB], axis=1).astype(np.float32)
    x = f(inp["x"])[0]
    xTs, cms = [], []
    for c in range(NCORES):
        xs = x[c * T:(c + 1) * T]
        xTs.append(np.ascontiguousarray(xs.reshape(T, KT, 128).transpose(2, 1, 0)))
        cm = np.zeros((128, 24), np.float32)
        cm[:, 0:8] = (np.arange(8) < c).astype(np.float32)[None, :]
        cm[:, 8:16] = (np.arange(8) == c - 1).astype(np.float32)[None, :]
        cm[:, 16:24] = (np.arange(8) >= c).astype(np.float32)[None, :]
        cms.append(cm)
    shared = {"small": small, "gw2": gw2, "cst": cst}
    return W, shared, xTs, cms


def _launch(steps, W, shared, xTs, cms, xall, ncores=NCORES):
    prog = Prog({"steps": steps})
    nc = prog.build()
    wnames = [n for n, _ in needed_weights(steps)]
    in_maps = []
    for c in range(ncores):
        m = dict(shared)
        for n in wnames:
            m[n] = W[n]
        m["xT"] = xTs[c]
        m["cm"] = cms[c]
        if prog.has_comb and not prog.fused:
            m["xall"] = xall
        in_maps.append(m)
    res = run_bass_kernel_spmd(nc, in_maps, core_ids=list(range(ncores)))
    return prog, res.results


def kernel(**inputs):
    W, shared, xTs, cms = _prep_inputs(inputs)
    xall = None
    out = np.empty((1, NCORES * T, D), np.float32)
    for steps in LAUNCH_STEPS:
        prog, results = _launch(steps, W, shared, xTs, cms, xall)
        if prog.has_final:
            for c in range(NCORES):
                oT = np.asarray(results[c]["outT"])
                out[0, c * T:(c + 1) * T] = oT.transpose(2, 1, 0).reshape(T, D)
        else:
            xTs = [np.asarray(results[c]["xT_out"]) for c in range(NCORES)]
            xall = np.concatenate([np.asarray(results[c]["xch"]) for c in range(NCORES)], axis=0)
    return out
```
